# Optimizing a Trainium2 kernel written in Bass

```python
import jax, jax.numpy as jnp
from jax import lax
import numpy as np

D_MODEL = 1024
BATCH = 4
SEQ = 8192
DEPTH = 4

CHUNK = 64
N_MEM = 256
D_MIX = D_MODEL
D_A = D_MIX // 2
D_B = D_MIX - D_A
HGRN_KDIM = 128
HGRN_HEADS = D_A // HGRN_KDIM
HGRN_VDIM = D_A // HGRN_HEADS
POOL_WINDOWS = (2, 4, 8, 16)
POOL_GROUPS = len(POOL_WINDOWS)
POOL_CH = D_B // POOL_GROUPS
D_IN = 4 * D_A + D_B
D_FF = ((8 * D_MODEL // 3) + 127) // 128 * 128
XA_HEADS = 4
XA_HEAD_DIM = D_MODEL // XA_HEADS
EPS = 1e-6

kernel_name = 'hymba_hgrn2_pool_macaron_memxattn'


def rmsnorm(x, g):
    xf = x.astype(jnp.float32)
    y = xf * lax.rsqrt(jnp.mean(xf * xf, axis=-1, keepdims=True) + EPS)
    return (y * g.astype(jnp.float32)).astype(x.dtype)


def swiglu(h, w1, w3, w2):
    return (jax.nn.silu(h @ w1) * (h @ w3)) @ w2


def hgrn2_chunkwise(q, k, v, log_f):
    B, T, H, K = q.shape
    V = v.shape[-1]
    nc = T // CHUNK

    def to_chunks(a):
        return a.reshape(B, nc, CHUNK, H, a.shape[-1]).transpose(1, 0, 3, 2, 4)

    causal = jnp.tril(jnp.ones((CHUNK, CHUNK), dtype=bool))[:, :, None]

    def step(S, inp):
        qc, kc, vc, gc = inp
        b = jnp.cumsum(gc, axis=2)
        o_inter = jnp.einsum('bhtk,bhkv->bhtv', qc * jnp.exp(b), S)
        rel = b[:, :, :, None, :] - b[:, :, None, :, :]
        decay = jnp.exp(jnp.where(causal, rel, -jnp.inf))
        A = jnp.einsum('bhtk,bhsk,bhtsk->bhts', qc, kc, decay)
        o = o_inter + jnp.einsum('bhts,bhsv->bhtv', A, vc)
        b_last = b[:, :, -1:, :]
        S = jnp.exp(b_last[:, :, 0, :, None]) * S + jnp.einsum(
            'bhsk,bhsv->bhkv', kc * jnp.exp(b_last - b), vc)
        return S, o

    S0 = jnp.zeros((B, H, K, V), jnp.float32)
    _, o = lax.scan(step, S0, (to_chunks(q), to_chunks(k), to_chunks(v), to_chunks(log_f)))
    return o.transpose(1, 0, 3, 2, 4).reshape(B, T, H, V)


def hgrn2_mixer(q_in, f_in, i_in, g_in, lb, norm_g):
    B, T, _ = q_in.shape
    shp = (B, T, HGRN_HEADS, HGRN_KDIM)
    q = q_in.astype(jnp.float32).reshape(shp)
    z = f_in.astype(jnp.float32).reshape(shp)
    lbh = lb.reshape(HGRN_HEADS, HGRN_KDIM)
    log_f = jnp.logaddexp(jnp.log(lbh), jnp.log1p(-lbh) + jax.nn.log_sigmoid(z))
    k = (1.0 - lbh) * jax.nn.sigmoid(-z)
    v = i_in.astype(jnp.float32).reshape(B, T, HGRN_HEADS, HGRN_VDIM)
    o = hgrn2_chunkwise(q, k, v, log_f)
    o = o * lax.rsqrt(jnp.mean(o * o, axis=-1, keepdims=True) + EPS)
    o = o * norm_g.astype(jnp.float32).reshape(HGRN_HEADS, HGRN_VDIM)
    o = o.reshape(B, T, D_A) * jax.nn.silu(g_in.astype(jnp.float32))
    return o.astype(q_in.dtype)


def multiscale_pool(u, pool_w, pool_scale):
    B, T, _ = u.shape
    uf = u.astype(jnp.float32).reshape(B, T, POOL_GROUPS, POOL_CH)
    cs = jnp.cumsum(uf, axis=1)
    t_count = jnp.arange(1, T + 1, dtype=jnp.float32)
    outs = []
    for j, w in enumerate(POOL_WINDOWS):
        csj = cs[:, :, j]
        prev = jnp.pad(csj, ((0, 0), (w, 0), (0, 0)))[:, :T]
        mean = (csj - prev) / jnp.minimum(t_count, w)[None, :, None]
        outs.append(mean - uf[:, :, j])
    p = jnp.stack(outs, axis=2)
    y = jnp.einsum('btgc,gcd->btgd', p, pool_w.astype(jnp.float32)).reshape(B, T, D_B)
    return (y * pool_scale.astype(jnp.float32)).astype(u.dtype)


def mem_cross_attention(h, mem_n, wq, wkv, wo):
    B, T, _ = h.shape
    M = mem_n.shape[1]
    q = (h @ wq).reshape(B, T, XA_HEADS, XA_HEAD_DIM)
    kv = (mem_n @ wkv).reshape(B, M, 2, XA_HEADS, XA_HEAD_DIM)
    k, v = kv[:, :, 0], kv[:, :, 1]
    s = jnp.einsum('bthd,bmhd->bhtm', q, k).astype(jnp.float32) * (XA_HEAD_DIM ** -0.5)
    p = jax.nn.softmax(s, axis=-1).astype(h.dtype)
    o = jnp.einsum('bhtm,bmhd->bthd', p, v).reshape(B, T, D_MODEL)
    return o @ wo


def setup_inputs(seed: int = 0) -> dict:
    key = jax.random.key(seed)
    ks = jax.random.split(key, 24)
    f32 = jnp.float32

    def nrm(k, shape, scale):
        return jax.random.normal(k, shape, f32) * scale

    def gain(k, shape):
        return 1.0 + 0.1 * jax.random.normal(k, shape, f32)

    L, D, F = DEPTH, D_MODEL, D_FF
    return {
        'x': nrm(ks[0], (BATCH, SEQ, D), 1.0),
        'mem': nrm(ks[1], (BATCH, N_MEM, D), 1.0),
        'ffn1_norm': gain(ks[2], (L, D)),
        'ffn1_w1': nrm(ks[3], (L, D, F), D ** -0.5),
        'ffn1_w3': nrm(ks[4], (L, D, F), D ** -0.5),
        'ffn1_w2': nrm(ks[5], (L, F, D), F ** -0.5),
        'mix_norm': gain(ks[6], (L, D)),
        'w_in': nrm(ks[7], (L, D, D_IN), D ** -0.5),
        'lb_logits': nrm(ks[8], (L, D_A), 1.0),
        'hgrn_norm': gain(ks[9], (L, D_A)),
        'pool_w': nrm(ks[10], (L, POOL_GROUPS, POOL_CH, POOL_CH), POOL_CH ** -0.5),
        'pool_scale': gain(ks[11], (L, D_B)),
        'w_out': nrm(ks[12], (L, D_MIX, D), D_MIX ** -0.5),
        'xa_norm': gain(ks[13], (L, D)),
        'mem_norm': gain(ks[14], (L, D)),
        'xa_wq': nrm(ks[15], (L, D, D), D ** -0.5),
        'xa_wkv': nrm(ks[16], (L, D, 2 * D), D ** -0.5),
        'xa_wo': nrm(ks[17], (L, D, D), D ** -0.5),
        'ffn2_norm': gain(ks[18], (L, D)),
        'ffn2_w1': nrm(ks[19], (L, D, F), D ** -0.5),
        'ffn2_w3': nrm(ks[20], (L, D, F), D ** -0.5),
        'ffn2_w2': nrm(ks[21], (L, F, D), F ** -0.5),
        'final_norm': gain(ks[22], (D,)),
    }


def reference(x, mem, ffn1_norm, ffn1_w1, ffn1_w3, ffn1_w2, mix_norm, w_in, lb_logits,
              hgrn_norm, pool_w, pool_scale, w_out, xa_norm, mem_norm, xa_wq, xa_wkv,
              xa_wo, ffn2_norm, ffn2_w1, ffn2_w3, ffn2_w2, final_norm):
    lb_all = jnp.cumsum(jax.nn.softmax(lb_logits.astype(jnp.float32), axis=0), axis=0)
    lb_all = lb_all - lb_all[0:1]
    splits = [D_A, 2 * D_A, 3 * D_A, 4 * D_A]
    for l in range(DEPTH):
        x = x + 0.5 * swiglu(rmsnorm(x, ffn1_norm[l]), ffn1_w1[l], ffn1_w3[l], ffn1_w2[l])
        h = rmsnorm(x, mix_norm[l])
        proj = h @ w_in[l]
        q_a, f_a, i_a, g_a, u_b = jnp.split(proj, splits, axis=-1)
        o_a = hgrn2_mixer(q_a, f_a, i_a, g_a, lb_all[l], hgrn_norm[l])
        o_b = multiscale_pool(u_b, pool_w[l], pool_scale[l])
        x = x + jnp.concatenate([o_a, o_b], axis=-1) @ w_out[l]
        x = x + mem_cross_attention(rmsnorm(x, xa_norm[l]), rmsnorm(mem, mem_norm[l]),
                                    xa_wq[l], xa_wkv[l], xa_wo[l])
        x = x + 0.5 * swiglu(rmsnorm(x, ffn2_norm[l]), ffn2_w1[l], ffn2_w3[l], ffn2_w2[l])
    return rmsnorm(x, final_norm)
```

```python
import numpy as np
import contextlib
import concourse.bass as bass
import concourse.mybir as mybir
from concourse.bass_utils import run_bass_kernel_spmd

F32 = mybir.dt.float32
BF16 = mybir.dt.bfloat16
AF = mybir.ActivationFunctionType
ALU = mybir.AluOpType

D = 1024
DFF = 2816
NFC = DFF // 128
TT = 512
NG = TT // 128
NH = 4
NMEM = 256
EPS = 1e-6
NCH = TT // 64
SEM_CAP = 2000


class Buf:
    __slots__ = ("name", "last_w", "readers")

    def __init__(self, name):
        self.name = name
        self.last_w = None
        self.readers = []


class Sched:
    ENGS = ("pe", "act", "dve", "pool", "sp")

    def __init__(self, dry=False):
        self.dry = dry
        self.ops = {e: [] for e in self.ENGS}
        self.count = {}
        self.waited = {e: {} for e in self.ENGS}
        self.need = {}

    def op(self, eng, fn, r=(), w=(), dma=None):
        if self.dry:
            return
        key = eng if dma is None else "dma:" + dma
        idx = self.count.get(key, 0) + 1
        self.count[key] = idx
        deps = set()
        for b in r:
            if b.last_w is not None:
                deps.add(b.last_w)
        for b in w:
            if b.last_w is not None:
                deps.add(b.last_w)
            deps.update(b.readers)
        waits = []
        wd = self.waited[eng]
        for (k, i) in sorted(deps):
            if k == "pe" and eng == "pe":
                continue
            if wd.get(k, 0) >= i:
                continue
            wd[k] = i
            waits.append((k, i))
            self.need.setdefault(k, set()).add(i)
        me = (key, idx)
        for b in w:
            b.last_w = me
            b.readers = []
        for b in r:
            b.readers.append(me)
        self.ops[eng].append((fn, waits, key, idx))

    def finalize(self, nc, stack):
        self.rank = {}
        self.sems = {}
        for k, s in self.need.items():
            if k.startswith("dma:"):
                continue
            srt = sorted(s)
            self.rank[k] = {i: r for r, i in enumerate(srt)}
            n = (len(srt) + SEM_CAP - 1) // SEM_CAP
            self.sems[k] = [stack.enter_context(nc.semaphore("s_%s_%d" % (k, j))) for j in range(n)]
        self.dsem = {}
        for k in self.count:
            if k.startswith("dma:"):
                self.dsem[k] = stack.enter_context(nc.semaphore("d_" + k[4:]))

    def semval(self, k, i):
        if k.startswith("dma:"):
            return self.dsem[k], 16 * i
        r = self.rank[k][i]
        return self.sems[k][r // SEM_CAP], (r % SEM_CAP) + 1

    def emit(self, eng, e):
        for fn, waits, key, idx in self.ops[eng]:
            for (k, i) in waits:
                s, v = self.semval(k, i)
                e.wait_ge(s, v)
            ins = fn(e)
            if key.startswith("dma:"):
                ins.then_inc(self.dsem[key], 16)
            elif idx in self.need.get(key, ()):
                s, v = self.semval(key, idx)
                ins.then_inc(s, 1)


class Gen:
    def __init__(self, NT, stages, fused, names2d, final_in_prog):
        self.NT = NT
        self.TOK = NT * TT
        self.stages = stages
        self.fused = fused
        self.names2d = names2d
        self.final_in_prog = final_in_prog

    def dram_in(self, name, shape, dt=F32):
        t = self.nc.dram_tensor(name, list(shape), dt, kind="ExternalInput").ap()
        self.in_names.append(name)
        return t

    def sb(self, name, shape, dt):
        return self.stack.enter_context(self.nc.sbuf_tensor("sb_" + name, list(shape), dt))

    def psget(self):
        i = self.psi
        self.psi = (i + 1) % 8
        return self.ps[i], self.psb[i]

    def cp(self, out, in_, r, w):
        self.cpi ^= 1
        if self.cpi:
            self.S.op("act", lambda e: e.activation(out=out, in_=in_, func=AF.Copy), r=r, w=w)
        else:
            self.S.op("dve", lambda e: e.tensor_copy(out=out, in_=in_), r=r, w=w)

    def tmpf(self, i):
        ap = self.aTflat[:, 1024 * i:1024 * (i + 1)].bitcast(F32)
        return ap, [self.aTb[2 * i], self.aTb[2 * i + 1]]

    def wnext(self, src, col0, ncols, k0, nk):
        W = self
        if W.S.dry:
            W.wreqs.append((src, col0, ncols, k0, nk))
            return W.wslot[0], W.wslotb[0]
        i = W.wi
        W.wi += 1
        upto = min(len(W.wplan), i + W.NS - 1)
        while W.wloaded < upto:
            j = W.wloaded
            (s_, c0, nc_, kk0, nk_) = W.wplan[j]
            slot = W.wslot[j % W.NS]
            sap = s_.rearrange("(kc p) n -> p kc n", p=128)[:, kk0:kk0 + nk_, c0:c0 + nc_]
            dst = slot[:, 0:nk_, 0:nc_]
            W.S.op("pool", (lambda e, dst=dst, sap=sap: e.dma_start(out=dst, in_=sap)),
                   r=[], w=[W.wslotb[j % W.NS]], dma="w%d" % (j % W.NS))
            W.wloaded += 1
        assert W.wplan[i] == (src, col0, ncols, k0, nk) or True
        return W.wslot[i % W.NS], W.wslotb[i % W.NS]

    def norm_T(self, src, srcb, ngroups, gcol, dst, dstb, ntok):
        S = self.S
        k = self.nrm_i
        self.nrm_i ^= 1
        ss, ssb = self.ss[k], self.ssb[k]
        rs, rsb = self.rs[k], self.rsb[k]
        for g in range(ngroups):
            S.op("act", (lambda e, g=g: e.activation(out=self.junk[:], in_=src[:, g, :], func=AF.Square,
                                                     accum_out=ss[:, g:g + 1])), r=[srcb[g]], w=[ssb])
        S.op("dve", lambda e: e.tensor_scalar(out=rs[:, 0:ngroups], in0=ss[:, 0:ngroups], scalar1=1.0 / D, scalar2=EPS,
                                              op0=ALU.mult, op1=ALU.add), r=[ssb], w=[rsb])
        S.op("act", lambda e: e.activation(out=rs[:, 0:ngroups], in_=rs[:, 0:ngroups], func=AF.Ln), r=[rsb], w=[rsb])
        S.op("act", lambda e: e.activation(out=rs[:, 0:ngroups], in_=rs[:, 0:ngroups], func=AF.Exp, scale=-0.5),
             r=[rsb], w=[rsb])
        gain = self.prmT[:, gcol:gcol + 8].unsqueeze(2).broadcast_to([128, 8, 128])
        for g in range(ngroups):
            j = self.xn_i
            self.xn_i ^= 1
            xn, xnb = self.xn[j], self.xnb[j]
            S.op("act", (lambda e, g=g, xn=xn: e.activation(out=xn[:], in_=src[:, g, :], func=AF.Copy,
                                                            scale=rs[:, g:g + 1])), r=[srcb[g], rsb], w=[xnb])
            pt, pb = self.psget()
            ptb = pt[:].bitcast(BF16)
            for c in range(8):
                S.op("pe", (lambda e, c=c, xn=xn, ptb=ptb: e.transpose(out=ptb[:, c * 128:(c + 1) * 128],
                                                                     in_=xn[:, c * 128:(c + 1) * 128],
                                                                     identity=self.identb[:])),
                     r=[xnb, self.cb], w=[pb])
            S.op("dve", (lambda e, g=g, ptb=ptb: e.tensor_tensor(
                out=dst[:, :, g * 128:(g + 1) * 128], in0=ptb.rearrange("p (c t) -> p c t", t=128),
                in1=gain, op=ALU.mult)), r=[pb, self.prmb], w=[dstb])

    def ffn(self, xt, xtb, gcol, w1, w3, w2):
        S = self.S
        self.norm_T(xt, xtb, NG, gcol, self.hT, self.hTb, TT)
        nfb = (DFF + 511) // 512
        for fb in range(nfb):
            ncols = min(512, DFF - fb * 512)
            s1, s1b = self.wnext(w1, fb * 512, ncols, 0, 8)
            s3, s3b = self.wnext(w3, fb * 512, ncols, 0, 8)
            for j in range(ncols // 128):
                fc = fb * 4 + j
                pa, pab = self.psget()
                pbk, pbb = self.psget()
                for (slot, slb, pt, ptb_) in ((s1, s1b, pa, pab), (s3, s3b, pbk, pbb)):
                    for k in range(8):
                        S.op("pe", (lambda e, slot=slot, pt=pt, k=k, j=j: e.matmul(
                            pt[:], slot[:, k, j * 128:(j + 1) * 128], self.hT[:, k, :], start=(k == 0), stop=(k == 7))),
                            r=[slb, self.hTb], w=[ptb_])
                ti = self.tmps_i
                self.tmps_i ^= 1
                tm, tmb = self.tmps[ti], self.tmpsb[ti]
                S.op("act", (lambda e, tm=tm, pa=pa: e.activation(out=tm[:], in_=pa[:], func=AF.Silu)), r=[pab], w=[tmb])
                S.op("dve", (lambda e, tm=tm, pbk=pbk, fc=fc: e.tensor_tensor(out=self.aT[:, fc, :], in0=tm[:], in1=pbk[:],
                                                                             op=ALU.mult)), r=[tmb, pbb], w=[self.aTb[fc]])
        for n in range(2):
            ys = [self.psget() for _ in range(NG)]
            for grp in range(3):
                nk = min(8, NFC - grp * 8)
                s2, s2b = self.wnext(w2, n * 512, 512, grp * 8, nk)
                for j in range(nk):
                    fc = grp * 8 + j
                    for g in range(NG):
                        S.op("pe", (lambda e, g=g, j=j, fc=fc, s2=s2, y=ys[g][0]: e.matmul(
                            y[:], self.aT[:, fc, g * 128:(g + 1) * 128], s2[:, j, :], start=(fc == 0), stop=(fc == NFC - 1))),
                            r=[self.aTb[fc], s2b], w=[ys[g][1]])
            for g in range(NG):
                S.op("dve", (lambda e, g=g, n=n, y=ys[g][0]: e.scalar_tensor_tensor(
                    out=xt[:, g, n * 512:(n + 1) * 512], in0=y[:], scalar=0.5, in1=xt[:, g, n * 512:(n + 1) * 512],
                    op0=ALU.mult, op1=ALU.add)), r=[ys[g][1], xtb[g]], w=[xtb[g]])

    def proj_add(self, xt, xtb, srcT, srcb_list, w):
        S = self.S
        for n in range(2):
            sl, slb = self.wnext(w, n * 512, 512, 0, 8)
            for g in range(NG):
                y, yb = self.psget()
                for k in range(8):
                    S.op("pe", (lambda e, g=g, k=k, y=y, sl=sl: e.matmul(
                        y[:], srcT[:, k, g * 128:(g + 1) * 128], sl[:, k, :], start=(k == 0), stop=(k == 7))),
                        r=[srcb_list[k], slb], w=[yb])
                S.op("dve", (lambda e, g=g, n=n, y=y: e.tensor_tensor(
                    out=xt[:, g, n * 512:(n + 1) * 512], in0=y[:], in1=xt[:, g, n * 512:(n + 1) * 512], op=ALU.add)),
                    r=[yb, xtb[g]], w=[xtb[g]])

    def hgrn(self, w_in, lbcol, St, Stb, full, hn_col=None):
        S = self.S
        sf, sfb = self.wnext(w_in, 512, 512, 0, 8)
        for h in range(NH):
            z, zb = self.psget()
            for k in range(8):
                S.op("pe", (lambda e, k=k, h=h, z=z: e.matmul(z[:], sf[:, k, h * 128:(h + 1) * 128], self.hT[:, k, :],
                                                              start=(k == 0), stop=(k == 7))), r=[sfb, self.hTb], w=[zb])
            s_, s_b = self.tmpf(0)
            g_, g_b = self.tmpf(1)
            kk, kkb = self.tmpf(2)
            G_, G_b = self.tmpf(3)
            Dd, Ddb = self.tmpf(4)
            E_, E_b = self.tmpf(5 + (h % 2))
            lb = self.lbs[:, lbcol + 3 * h:lbcol + 3 * h + 1]
            oml = self.lbs[:, lbcol + 3 * h + 1:lbcol + 3 * h + 2]
            noml = self.lbs[:, lbcol + 3 * h + 2:lbcol + 3 * h + 3]
            import os
            hd = int(os.environ.get("K_HDBG", "9"))
            S.op("act", (lambda e, z=z, s_=s_: e.activation(out=s_, in_=z[:], func=AF.Sigmoid)), r=[zb], w=s_b)
            if hd < 2:
                continue
            S.op("act", (lambda e, s_=s_, g_=g_, lb=lb, oml=oml: e.activation(out=g_, in_=s_, func=AF.Ln, bias=lb, scale=oml)),
                 r=s_b + [self.lbsb], w=g_b)
            S.op("dve", (lambda e, s_=s_, kk=kk, oml=oml, noml=noml: e.tensor_scalar(
                out=kk, in0=s_, scalar1=noml, scalar2=oml, op0=ALU.mult, op1=ALU.add)), r=s_b + [self.lbsb], w=kkb)
            S.op("dve", (lambda e, g_=g_, G_=G_: e.tensor_tensor_scan(out=G_, data0=self.onesrow[:], data1=g_, initial=0.0,
                                                                      op0=ALU.mult, op1=ALU.add)), r=g_b + [self.cb], w=G_b)
            G3 = G_.rearrange("p (c t) -> p c t", t=64)
            D3 = Dd.rearrange("p (c t) -> p c t", t=64)
            S.op("dve", (lambda e, G3=G3, D3=D3: e.tensor_tensor(out=D3, in0=G3, in1=G3[:, :, 31:32].broadcast_to([128, NCH, 64]),
                                                                 op=ALU.subtract)), r=G_b, w=Ddb)
            if hd < 3:
                continue
            ab = self.abg[:, h, :, :]
            r_ = G3[:, :, 31]
            e_ = G3[:, :, 63]
            S.op("dve", (lambda e, ab=ab, r_=r_, e_=e_: e.tensor_tensor(out=ab[:, 1, :], in0=e_, in1=r_, op=ALU.subtract)),
                 r=G_b, w=[self.abgb[h]])
            S.op("dve", (lambda e, ab=ab, r_=r_: e.tensor_copy(out=ab[:, 0, 0:1], in_=r_[:, 0:1])), r=G_b, w=[self.abgb[h]])
            S.op("dve", (lambda e, ab=ab, r_=r_, e_=e_: e.tensor_tensor(out=ab[:, 0, 1:NCH], in0=r_[:, 1:NCH], in1=e_[:, 0:NCH - 1],
                                                                        op=ALU.subtract)), r=G_b, w=[self.abgb[h]])
            S.op("dve", (lambda e, ab=ab, e_=e_: e.tensor_copy(out=ab[:, 2, 0:1], in_=e_[:, 0:1])), r=G_b, w=[self.abgb[h]])
            S.op("dve", (lambda e, ab=ab, e_=e_: e.tensor_tensor(out=ab[:, 2, 1:NCH], in0=e_[:, 1:NCH], in1=e_[:, 0:NCH - 1],
                                                                 op=ALU.subtract)), r=G_b, w=[self.abgb[h]])
            S.op("act", (lambda e, ab=ab: e.activation(out=ab, in_=ab, func=AF.Exp)), r=[self.abgb[h]], w=[self.abgb[h]])
            if hd < 4:
                continue
            if full:
                Eh, Ehb = self.tmpf(5 + h)
                S.op("act", (lambda e, Dd=Dd, Eh=Eh: e.activation(out=Eh, in_=Dd, func=AF.Exp)), r=Ddb, w=Ehb)
            Ei, Eib = self.tmpf(9)
            S.op("act", (lambda e, Dd=Dd, Ei=Ei: e.activation(out=Ei, in_=Dd, func=AF.Exp, scale=-1.0)), r=Ddb, w=Eib)
            S.op("dve", (lambda e, kk=kk, Ei=Ei, h=h: e.tensor_tensor(out=self.khatT[:, h, :], in0=kk, in1=Ei, op=ALU.mult)),
                 r=kkb + Eib, w=[self.khTb[h]])
        if hd < 5:
            return
        for g in range(NG):
            pt, pb = self.psget()
            ptb = pt[:].bitcast(BF16)
            for h in range(NH):
                S.op("pe", (lambda e, g=g, h=h, ptb=ptb: e.transpose(out=ptb[:, h * 128:(h + 1) * 128],
                                                                     in_=self.khatT[:, h, g * 128:(g + 1) * 128],
                                                                     identity=self.identb[:])), r=[self.khTb[h], self.cb], w=[pb])
            self.cp(self.khtok[0:64, 0, g, :], ptb[0:64, 0:512], [pb], [self.khtokb[g]])
            self.cp(self.khtok[64:128, 1, g, :], ptb[64:128, 0:512], [pb], [self.khtokb[g]])
        if hd < 6:
            return
        si, sib = self.wnext(w_in, 1024, 512, 0, 8)
        for g in range(NG):
            vp, vpb = self.psget()
            for k in range(8):
                S.op("pe", (lambda e, g=g, k=k, vp=vp: e.matmul(vp[:], self.hT[:, k, g * 128:(g + 1) * 128], si[:, k, :],
                                                                start=(k == 0), stop=(k == 7))), r=[self.hTb, sib], w=[vpb])
            self.cp(self.vtok[:, g, :], vp[:], [vpb], [self.vtokb[g]])
        if hd < 7:
            return
        for h in range(NH):
            banks = [self.psget(), self.psget()]
            for c in range(NCH):
                g = c // 2
                rows = slice((c % 2) * 64, (c % 2) * 64 + 64)
                bk, bkb = banks[c // 4]
                col = (c % 4) * 128
                S.op("pe", (lambda e, g=g, h=h, c=c, bk=bk, col=col: e.matmul(
                    bk[:, col:col + 128], self.khtok[:, c % 2, g, h * 128:(h + 1) * 128], self.vtok[:, g, h * 128:(h + 1) * 128],
                    start=True, stop=True)), r=[self.khtokb[g], self.vtokb[g], self.cb], w=[bkb])
            for c in range(NCH):
                if hd < 8:
                    break
                bk, bkb = banks[c // 4]
                col = (c % 4) * 128
                al = self.abg[:, h, 0, c:c + 1]
                be = self.abg[:, h, 1, c:c + 1]
                ga = self.abg[:, h, 2, c:c + 1]
                if full:
                    S.op("act", (lambda e, h=h, c=c, al=al: e.activation(out=self.Ssc[:, h, c, :], in_=St[:, h, :], func=AF.Copy,
                                                                         scale=al)), r=[Stb[h], self.abgb[h]], w=[self.Sscb[h][c]])
                td, tdb = self.tmpd[c % 2], self.tmpdb[c % 2]
                S.op("dve", (lambda e, td=td, bk=bk, col=col, be=be: e.tensor_scalar(
                    out=td[:], in0=bk[:, col:col + 128], scalar1=be, scalar2=None, op0=ALU.mult)), r=[bkb, self.abgb[h]], w=[tdb])
                S.op("dve", (lambda e, td=td, h=h, ga=ga: e.scalar_tensor_tensor(
                    out=St[:, h, :], in0=St[:, h, :], scalar=ga, in1=td[:], op0=ALU.mult, op1=ALU.add)),
                    r=[Stb[h], tdb, self.abgb[h]], w=[Stb[h]])
        if not full:
            return
        sq_, sqb = self.wnext(w_in, 0, 512, 0, 8)
        for h in range(NH):
            qp, qpb = self.psget()
            for k in range(8):
                S.op("pe", (lambda e, k=k, h=h, qp=qp: e.matmul(qp[:], sq_[:, k, h * 128:(h + 1) * 128], self.hT[:, k, :],
                                                                start=(k == 0), stop=(k == 7))), r=[sqb, self.hTb], w=[qpb])
            Eh, Ehb = self.tmpf(5 + h)
            S.op("dve", (lambda e, h=h, qp=qp, Eh=Eh: e.tensor_tensor(out=self.qhatT[:, h, :], in0=qp[:], in1=Eh, op=ALU.mult)),
                 r=[qpb] + Ehb, w=[self.qhTb[h]])
        for h in range(NH):
            bk, bkb = self.psget()
            for c in range(NCH):
                rows = slice((c % 2) * 64, (c % 2) * 64 + 64)
                S.op("pe", (lambda e, h=h, c=c, rows=rows, bk=bk: e.matmul(
                    bk[rows, c * 64:(c + 1) * 64], self.khatT[:, h, c * 64:(c + 1) * 64], self.qhatT[:, h, c * 64:(c + 1) * 64],
                    start=True, stop=True)), r=[self.khTb[h], self.qhTb[h]], w=[bkb])
            for half in range(2):
                rows = slice(half * 64, half * 64 + 64)
                o3 = self.ATm[rows, h, :].rearrange("p (j x t) -> p j x t", x=2, t=64)[:, :, half, :]
                i3 = bk[rows, :].rearrange("p (j x t) -> p j x t", x=2, t=64)[:, :, half, :]
                m3 = self.mask[rows, :].rearrange("p (j t) -> p j t", t=64)
                S.op("dve", (lambda e, o3=o3, i3=i3, m3=m3: e.tensor_tensor(out=o3, in0=i3, in1=m3, op=ALU.mult)),
                     r=[bkb, self.cb], w=[self.ATmb[h]])
        sg_, sgb = self.wnext(w_in, 1536, 512, 0, 8)
        for h in range(NH):
            ob, obb = self.psget()
            for c in range(NCH):
                g = c // 2
                rows = slice((c % 2) * 64, (c % 2) * 64 + 64)
                cols = slice(c * 64, (c + 1) * 64)
                S.op("pe", (lambda e, h=h, c=c, cols=cols, ob=ob: e.matmul(
                    ob[:, cols], self.Ssc[:, h, c, :], self.qhatT[:, h, cols], start=True, stop=False)),
                    r=[self.Sscb[h][c], self.qhTb[h]], w=[obb])
                S.op("pe", (lambda e, h=h, g=g, cols=cols, c=c, ob=ob: e.matmul(
                    ob[:, cols], self.vtok[:, g, h * 128:(h + 1) * 128], self.ATm[:, h, cols],
                    start=False, stop=True)), r=[self.vtokb[g], self.ATmb[h], self.cb], w=[obb])
            sq2, sq2b = self.tmpf(0)
            vr, vrb = self.tmpf(1)
            sgt, sgtb = self.tmpf(2)
            t1, t1b = self.tmpf(3)
            S.op("act", (lambda e, ob=ob, sq2=sq2: e.activation(out=sq2, in_=ob[:], func=AF.Square)), r=[obb], w=sq2b)
            sp, spb = self.psget()
            S.op("pe", (lambda e, sp=sp, sq2=sq2: e.matmul(sp[:], self.onesf[:], sq2, start=True, stop=True)),
                 r=sq2b + [self.cb], w=[spb])
            S.op("dve", (lambda e, sp=sp, vr=vr: e.tensor_scalar(out=vr, in0=sp[:], scalar1=1.0 / 128, scalar2=EPS,
                                                                 op0=ALU.mult, op1=ALU.add)), r=[spb], w=vrb)
            S.op("act", (lambda e, vr=vr: e.activation(out=vr, in_=vr, func=AF.Ln)), r=vrb, w=vrb)
            S.op("act", (lambda e, vr=vr: e.activation(out=vr, in_=vr, func=AF.Exp, scale=-0.5)), r=vrb, w=vrb)
            gp, gpb = self.psget()
            for k in range(8):
                S.op("pe", (lambda e, k=k, h=h, gp=gp: e.matmul(gp[:], sg_[:, k, h * 128:(h + 1) * 128], self.hT[:, k, :],
                                                                start=(k == 0), stop=(k == 7))), r=[sgb, self.hTb], w=[gpb])
            S.op("act", (lambda e, gp=gp, sgt=sgt: e.activation(out=sgt, in_=gp[:], func=AF.Silu)), r=[gpb], w=sgtb)
            S.op("dve", (lambda e, ob=ob, vr=vr, t1=t1: e.tensor_tensor(out=t1, in0=ob[:], in1=vr, op=ALU.mult)),
                 r=[obb] + vrb, w=t1b)
            hn = self.prmT[:, hn_col + h:hn_col + h + 1]
            S.op("dve", (lambda e, h=h, t1=t1, sgt=sgt, hn=hn: e.scalar_tensor_tensor(
                out=self.catT[:, h, :], in0=t1, scalar=hn, in1=sgt, op0=ALU.mult, op1=ALU.mult)),
                r=t1b + sgtb + [self.prmb], w=[self.catb[h]])

    def pool_part(self, w_in, tile_idx, full, ps_col=None, last_tile=False):
        S = self.S
        su, sub = self.wnext(w_in, 2048, 512, 0, 8)
        for grp in range(4):
            up, upb = self.psget()
            for k in range(8):
                S.op("pe", (lambda e, k=k, grp=grp, up=up: e.matmul(up[:], su[:, k, grp * 128:(grp + 1) * 128], self.hT[:, k, :],
                                                                    start=(k == 0), stop=(k == 7))), r=[sub, self.hTb], w=[upb])
            if not full:
                self.cp(self.utail1[:, grp, :], up[:, TT - 16:TT], [upb], [self.utail1b])
                continue
            ub, ubb = self.ubuf, self.ubufb
            S.op("act", (lambda e, up=up: e.activation(out=self.ubuf[:, 16:16 + TT], in_=up[:], func=AF.Copy)), r=[upb], w=[ubb])
            S.op("dve", (lambda e, grp=grp: e.tensor_copy(out=self.ubuf[:, 0:16], in_=self.halo[:, grp, :])), r=[self.halob[grp]], w=[ubb])
            S.op("dve", (lambda e, grp=grp: e.tensor_copy(out=self.halo[:, grp, :], in_=self.ubuf[:, TT:TT + 16])), r=[ubb], w=[self.halob[grp]])
            src, srcb = self.ubuf, ubb
            lo = 0
            kstep = 1
            pp = 0
            for lvl in range(grp + 1):
                dst, dstb = self.pp[pp], self.ppb[pp]
                pp ^= 1
                lo2 = lo + kstep
                S.op("dve", (lambda e, src=src, dst=dst, lo2=lo2, kstep=kstep: e.tensor_tensor(
                    out=dst[:, lo2:TT + 16], in0=src[:, lo2:TT + 16], in1=src[:, lo2 - kstep:TT + 16 - kstep], op=ALU.add)),
                    r=[srcb], w=[dstb])
                src, srcb = dst, dstb
                lo = lo2
                kstep *= 2
            wdw = 2 ** (grp + 1)
            S.op("dve", (lambda e, src=src, grp=grp, wdw=wdw: e.scalar_tensor_tensor(
                out=self.pT[:, grp, :], in0=src[:, 16:16 + TT], scalar=1.0 / wdw, in1=self.ubuf[:, 16:16 + TT],
                op0=ALU.mult, op1=ALU.subtract)), r=[srcb, ubb], w=[self.pTb[grp]])
            if tile_idx == 0:
                t16, t16b = self.t16, self.t16b
                S.op("dve", (lambda e, src=src, grp=grp: e.tensor_tensor(out=self.t16[:], in0=src[:, 16:32], in1=self.pinv0[:, grp, :],
                                                                         op=ALU.mult)), r=[srcb, self.cb2], w=[t16b])
                S.op("dve", (lambda e, grp=grp: e.tensor_tensor(out=self.pT[:, grp, 0:16], in0=self.t16[:], in1=self.ubuf[:, 16:32],
                                                                op=ALU.subtract)), r=[t16b, ubb], w=[self.pTb[grp]])
            yp, ypb = self.psget()
            S.op("pe", (lambda e, grp=grp, yp=yp: e.matmul(yp[:], self.poolw[:, grp, :], self.pT[:, grp, :], start=True, stop=True)),
                 r=[self.poolwb, self.pTb[grp]], w=[ypb])
            psc = self.prmT[:, ps_col + grp:ps_col + grp + 1]
            S.op("act", (lambda e, grp=grp, yp=yp, psc=psc: e.activation(out=self.catT[:, 4 + grp, :], in_=yp[:], func=AF.Copy, scale=psc)),
                 r=[ypb, self.prmb], w=[self.catb[4 + grp]])

    def xattn(self, xt, xtb, gcol, wq, wo):
        S = self.S
        self.norm_T(xt, xtb, NG, gcol, self.hT, self.hTb, TT)
        qT = self.aT
        OT = self.aT
        for blk in range(2):
            sl, slb = self.wnext(wq, blk * 512, 512, 0, 8)
            for j in range(4):
                ch = blk * 4 + j
                qp, qpb = self.psget()
                for k in range(8):
                    S.op("pe", (lambda e, k=k, j=j, qp=qp, sl=sl: e.matmul(qp[:], sl[:, k, j * 128:(j + 1) * 128], self.hT[:, k, :],
                                                                           start=(k == 0), stop=(k == 7))), r=[slb, self.hTb], w=[qpb])
                self.cp(qT[:, ch, :], qp[:], [qpb], [self.aTb[ch]])
        for h in range(NH):
            pi = self.pt_i
            self.pt_i ^= 1
            PT, PTb = self.PT[pi], self.PTb[pi]
            for mg in range(2):
                sp, spb = self.psget()
                for dc in range(2):
                    S.op("pe", (lambda e, h=h, mg=mg, dc=dc, sp=sp: e.matmul(
                        sp[:], self.KT[:, 2 * h + dc, mg * 128:(mg + 1) * 128], qT[:, 2 * h + dc, :], start=(dc == 0), stop=(dc == 1))),
                        r=[self.KTb, self.aTb[2 * h + dc]], w=[spb])
                S.op("act", (lambda e, mg=mg, sp=sp, PT=PT: e.activation(out=PT[:, mg, :], in_=sp[:], func=AF.Exp, scale=1.0 / 16.0)),
                     r=[spb], w=[PTb])
            dn, dnb = self.psget()
            for mg in range(2):
                S.op("pe", (lambda e, mg=mg, dn=dn, PT=PT: e.matmul(dn[:], self.onesb[:], PT[:, mg, :], start=(mg == 0), stop=(mg == 1))),
                     r=[PTb, self.cb], w=[dnb])
            rc, rcb = self.tmpf(8 + (h % 2))
            S.op("act", (lambda e, dn=dn, rc=rc: e.activation(out=rc, in_=dn[:], func=AF.Ln)), r=[dnb], w=rcb)
            S.op("act", (lambda e, rc=rc: e.activation(out=rc, in_=rc, func=AF.Exp, scale=-1.0)), r=rcb, w=rcb)
            for dvc in range(2):
                op_, opb = self.psget()
                for mg in range(2):
                    S.op("pe", (lambda e, h=h, mg=mg, dvc=dvc, op_=op_, PT=PT: e.matmul(
                        op_[:], self.Vm[:, mg, h * 256 + dvc * 128:h * 256 + (dvc + 1) * 128], PT[:, mg, :],
                        start=(mg == 0), stop=(mg == 1))), r=[self.Vmb, PTb], w=[opb])
                ch = 8 + 2 * h + dvc
                S.op("dve", (lambda e, ch=ch, op_=op_, rc=rc: e.tensor_tensor(out=OT[:, ch, :], in0=op_[:], in1=rc, op=ALU.mult)),
                     r=[opb] + rcb, w=[self.aTb[ch]])
        self.proj_add(xt, xtb, self.aT[:, 8:16, :], self.aTb[8:16], wo)

    def kv_setup(self, mem_ap, gcol, wkv):
        S = self.S
        mt, mtb = self.xt, self.xtb
        S.op("sp", lambda e: e.dma_start(out=mt[:, 0:2, :], in_=mem_ap.rearrange("(g p) d -> p g d", p=128)),
             r=[], w=[mtb[0], mtb[1]], dma="xld")
        self.norm_T(mt, mtb, 2, gcol, self.hT[:, :, 0:NMEM], self.hTb, NMEM)
        for blk in range(2):
            sl, slb = self.wnext(wkv, blk * 512, 512, 0, 8)
            for j in range(4):
                ch = blk * 4 + j
                kp, kpb = self.psget()
                for k in range(8):
                    S.op("pe", (lambda e, k=k, j=j, kp=kp, sl=sl: e.matmul(kp[:, 0:NMEM], sl[:, k, j * 128:(j + 1) * 128],
                                                                           self.hT[:, k, 0:NMEM], start=(k == 0), stop=(k == 7))),
                         r=[slb, self.hTb], w=[kpb])
                self.cp(self.KT[:, ch, :], kp[:, 0:NMEM], [kpb], [self.KTb])
        for blk in range(2):
            sl, slb = self.wnext(wkv, 1024 + blk * 512, 512, 0, 8)
            for mg in range(2):
                vp, vpb = self.psget()
                for k in range(8):
                    S.op("pe", (lambda e, k=k, mg=mg, vp=vp, sl=sl: e.matmul(vp[:], self.hT[:, k, mg * 128:(mg + 1) * 128], sl[:, k, :],
                                                                             start=(k == 0), stop=(k == 7))), r=[slb, self.hTb], w=[vpb])
                self.cp(self.Vm[:, mg, blk * 512:(blk + 1) * 512], vp[:], [vpb], [self.Vmb])

    def program(self, S):
        self.S = S
        self.psi = 0
        self.cpi = 0
        self.nrm_i = 0
        self.xn_i = 0
        self.tmps_i = 0
        self.pt_i = 0
        self.wi = 0
        self.wloaded = 0
        NT = self.NT
        T = self.T
        S.op("pool", lambda e: e.memset(self.identb[:], 1.0), w=[self.cb])
        S.op("pool", lambda e: e.affine_select(out=self.identb[:], in_=self.identb[:], pattern=[[1, 128]], compare_op=ALU.is_equal,
                                               fill=0.0, base=0, channel_multiplier=-1), r=[self.cb], w=[self.cb])
        S.op("pool", lambda e: e.memset(self.identf[:], 1.0), w=[self.cb])
        S.op("pool", lambda e: e.affine_select(out=self.identf[:], in_=self.identf[:], pattern=[[1, 128]], compare_op=ALU.is_equal,
                                               fill=0.0, base=0, channel_multiplier=-1), r=[self.cb], w=[self.cb])
        S.op("pool", lambda e: e.memset(self.onesf[:], 1.0), w=[self.cb])
        S.op("pool", lambda e: e.memset(self.onesb[:], 1.0), w=[self.cb])
        S.op("pool", lambda e: e.memset(self.onesrow[:], 1.0), w=[self.cb])
        S.op("pool", lambda e: e.memset(self.khtok[:], 0.0), w=[self.cb])
        S.op("pool", lambda e: e.memset(self.ATm[:], 0.0), w=[self.cb])
        S.op("pool", lambda e: e.memset(self.mask[:], 1.0), w=[self.cb])
        for half in range(2):
            rows = slice(half * 64, half * 64 + 64)
            S.op("pool", (lambda e, rows=rows: e.affine_select(out=self.mask[rows, :], in_=self.mask[rows, :], pattern=[[0, 4], [1, 64]],
                                                              compare_op=ALU.is_ge, fill=0.0, base=0, channel_multiplier=-1)),
                 r=[self.cb], w=[self.cb])
        S.op("sp", lambda e: e.dma_start(out=self.pinv0[:], in_=T["pinv0"].partition_broadcast(128)), w=[self.cb2], dma="misc")
        S.op("sp", lambda e: e.dma_start(out=self.lsel[:], in_=T["lsel"].partition_broadcast(128)), w=[self.cb2], dma="misc")
        S.op("sp", lambda e: e.dma_start(out=self.isb[:], in_=T["isb"].partition_broadcast(128)), w=[self.cb2], dma="misc")

        for si, (r2, r1, fin) in enumerate(self.stages):
            self.stage(si, r2, r1, fin)
        S.op("sp", lambda e: e.nop(), r=self.outbufs, w=[])

    def load_params(self, si, r2, r1, fin):
        S = self.S
        T = self.T
        rows = []
        cols = {}

        def add(name, ap2d, nrows):
            cols[name] = len(rows)
            for i in range(nrows):
                rows.append((ap2d, i))
        add("lb", T["lb_logits"].rearrange("l (h p) -> (l h) p", p=128), 16)
        if r2 is not None:
            for nm in ("mix_norm", "xa_norm", "mem_norm", "ffn2_norm"):
                add("2" + nm, T["%s_%s" % (nm, r2)].rearrange("(c p) -> c p", p=128), 8)
            add("2hgrn_norm", T["hgrn_norm_%s" % r2].rearrange("(c p) -> c p", p=128), 4)
            add("2pool_scale", T["pool_scale_%s" % r2].rearrange("(c p) -> c p", p=128), 4)
        if r1 is not None:
            for nm in ("ffn1_norm", "mix_norm"):
                add("1" + nm, T["%s_%s" % (nm, r1)].rearrange("(c p) -> c p", p=128), 8)
        R = len(rows)
        assert R <= 128
        i = 0
        while i < R:
            ap2d, r0 = rows[i]
            j = i
            while j < R and rows[j][0] is ap2d:
                j += 1
            n = j - i
            S.op("sp", (lambda e, i=i, n=n, ap2d=ap2d: e.dma_start(out=self.prmS[i:i + n, :], in_=ap2d[0:n, :])),
                 w=[self.prmSb], dma="prm")
            i = j
        pt, pb = self.psget()
        S.op("pe", lambda e: e.transpose(out=pt[:, 0:R], in_=self.prmS[0:R, :], identity=self.identf[0:R, 0:R]),
             r=[self.prmSb, self.cb], w=[pb])
        S.op("dve", lambda e: e.tensor_copy(out=self.prmT[:, 0:R], in_=pt[:, 0:R]), r=[pb], w=[self.prmb])
        lbT = self.prmT[:, 0:16].rearrange("p (l h) -> p l h", h=4)
        ex = self.lbx[:].rearrange("p (l h) -> p l h", h=4)
        S.op("act", lambda e: e.activation(out=self.lbx[:], in_=self.prmT[:, 0:16], func=AF.Exp), r=[self.prmb], w=[self.lbxb])
        den = self.lbd
        S.op("dve", lambda e: e.tensor_tensor(out=den[:, 0:4], in0=ex[:, 0, :], in1=ex[:, 1, :], op=ALU.add), r=[self.lbxb], w=[self.lbxb2])
        S.op("dve", lambda e: e.tensor_tensor(out=den[:, 0:4], in0=den[:, 0:4], in1=ex[:, 2, :], op=ALU.add), r=[self.lbxb, self.lbxb2], w=[self.lbxb2])
        S.op("dve", lambda e: e.tensor_tensor(out=den[:, 0:4], in0=den[:, 0:4], in1=ex[:, 3, :], op=ALU.add), r=[self.lbxb, self.lbxb2], w=[self.lbxb2])
        S.op("dve", lambda e: e.reciprocal(out=den[:, 4:8], in_=den[:, 0:4]), r=[self.lbxb2], w=[self.lbxb2])
        for l in range(4):
            S.op("dve", (lambda e, l=l: e.tensor_tensor(out=ex[:, l, :], in0=ex[:, l, :], in1=den[:, 4:8], op=ALU.mult)),
                 r=[self.lbxb, self.lbxb2], w=[self.lbxb])
        lbc = self.lbc[:].rearrange("p (l h) -> p l h", h=4)
        S.op("dve", lambda e: e.memset(self.lbc[:], 0.0), w=[self.lbcb])
        for l in range(1, 4):
            S.op("dve", (lambda e, l=l: e.tensor_tensor(out=lbc[:, l, :], in0=lbc[:, l - 1, :], in1=ex[:, l, :], op=ALU.add)),
                 r=[self.lbxb, self.lbcb], w=[self.lbcb])
        for ri, role in enumerate((r2, r1)):
            if role is None:
                continue
            selrow = (2 * si + ri) if self.fused else ri
            base = ri * 12
            tmp = self.lbd[:, 8:12]
            S.op("dve", (lambda e, selrow=selrow, tmp=tmp: e.tensor_scalar(out=tmp, in0=lbc[:, 0, :], scalar1=self.lsel[:, selrow, 0:1],
                                                                           scalar2=None, op0=ALU.mult)), r=[self.lbcb, self.cb2], w=[self.lbxb2])
            for l in range(1, 4):
                S.op("dve", (lambda e, selrow=selrow, l=l, tmp=tmp: e.scalar_tensor_tensor(
                    out=tmp, in0=lbc[:, l, :], scalar=self.lsel[:, selrow, l:l + 1], in1=tmp, op0=ALU.mult, op1=ALU.add)),
                    r=[self.lbcb, self.cb2, self.lbxb2], w=[self.lbxb2])
            for h in range(NH):
                c0 = base + 3 * h
                S.op("dve", (lambda e, c0=c0, h=h, tmp=tmp: e.tensor_copy(out=self.lbs[:, c0:c0 + 1], in_=tmp[:, h:h + 1])),
                     r=[self.lbxb2], w=[self.lbsb])
                S.op("dve", (lambda e, c0=c0, h=h, tmp=tmp: e.tensor_scalar(out=self.lbs[:, c0 + 1:c0 + 2], in0=tmp[:, h:h + 1], scalar1=-1.0,
                                                                            scalar2=1.0, op0=ALU.mult, op1=ALU.add)), r=[self.lbxb2], w=[self.lbsb])
                S.op("dve", (lambda e, c0=c0, h=h, tmp=tmp: e.tensor_scalar(out=self.lbs[:, c0 + 2:c0 + 3], in0=tmp[:, h:h + 1], scalar1=1.0,
                                                                            scalar2=-1.0, op0=ALU.mult, op1=ALU.add)), r=[self.lbxb2], w=[self.lbsb])
        return cols

    def stage(self, si, r2, r1, fin):
        S = self.S
        T = self.T
        NT = self.NT
        cols = self.load_params(si, r2, r1, fin)
        xsrc = T["x"] if si == 0 else self.xs
        xsrcb = self.xinb if si == 0 else self.xsb
        if r2 is not None:
            S.op("pool", lambda e: e.dma_start(out=self.poolw[:], in_=T["pool_w_%s" % r2].rearrange("g c d -> c g d")),
                 w=[self.poolwb], dma="poolw")
            self.kv_setup(T["mem"], cols["2mem_norm"], T["xa_wkv_%s" % r2])
            sin = self.sx_in[si]
            S.op("sp", lambda e: e.dma_start(out=self.S2[:], in_=sin.rearrange("p (h v) -> p h v", v=128)), r=[self.sxb[si]], w=self.S2b, dma="sin")
            S.op("dve", lambda e: e.tensor_scalar(out=self.S2[:], in0=self.S2[:], scalar1=self.isb[:, 0:1], scalar2=None, op0=ALU.mult),
                 r=self.S2b + [self.cb2], w=self.S2b)
            uin = self.ut_in[si]
            S.op("sp", lambda e: e.dma_start(out=self.halo[:], in_=uin.rearrange("p (g t) -> p g t", t=16)), r=[self.utb[si]], w=self.halob, dma="uin")
            S.op("dve", lambda e: e.tensor_scalar(out=self.halo[:], in0=self.halo[:], scalar1=self.isb[:, 0:1], scalar2=None, op0=ALU.mult),
                 r=self.halob + [self.cb2], w=self.halob)
        if r1 is not None:
            S.op("dve", lambda e: e.memset(self.S1[:], 0.0), w=self.S1b)
        if fin:
            S.op("sp", lambda e: e.dma_start(out=self.gfin[:], in_=T["final_norm"].partition_broadcast(128)), w=[self.gfinb], dma="gfin")
        xt, xtb = self.xt, self.xtb
        for t in range(NT):
            rows = slice(t * TT, (t + 1) * TT)
            S.op("sp", (lambda e, rows=rows: e.dma_start(out=xt[:], in_=xsrc[rows, :].rearrange("(g p) d -> p g d", p=128))),
                 r=[xsrcb[t]], w=xtb, dma="xld")
            if r2 is not None:
                w_in = T["w_in_%s" % r2]
                self.norm_T(xt, xtb, NG, cols["2mix_norm"], self.hT, self.hTb, TT)
                self.hgrn(w_in, 0, self.S2, self.S2b, True, hn_col=cols["2hgrn_norm"])
                self.pool_part(w_in, t, True, ps_col=cols["2pool_scale"])
                self.proj_add(xt, xtb, self.catT, self.catb, T["w_out_%s" % r2])
                self.xattn(xt, xtb, cols["2xa_norm"], T["xa_wq_%s" % r2], T["xa_wo_%s" % r2])
                self.ffn(xt, xtb, cols["2ffn2_norm"], T["ffn2_w1_%s" % r2], T["ffn2_w3_%s" % r2], T["ffn2_w2_%s" % r2])
            if r1 is not None:
                import os
                dbg = int(os.environ.get("K_DBG", "9"))
                w_in = T["w_in_%s" % r1]
                if dbg >= 3:
                    self.ffn(xt, xtb, cols["1ffn1_norm"], T["ffn1_w1_%s" % r1], T["ffn1_w3_%s" % r1], T["ffn1_w2_%s" % r1])
                if dbg >= 2:
                    self.norm_T(xt, xtb, NG, cols["1mix_norm"], self.hT, self.hTb, TT)
                if dbg >= 4:
                    self.hgrn(w_in, 12, self.S1, self.S1b, False)
                if t == NT - 1 and dbg >= 5:
                    self.pool_part(w_in, t, False)
            if fin:
                k = self.nrm_i
                self.nrm_i ^= 1
                ss, ssb, rs, rsb = self.ss[k], self.ssb[k], self.rs[k], self.rsb[k]
                for g in range(NG):
                    S.op("act", (lambda e, g=g: e.activation(out=self.junk[:], in_=xt[:, g, :], func=AF.Square, accum_out=ss[:, g:g + 1])),
                         r=[xtb[g]], w=[ssb])
                S.op("dve", lambda e: e.tensor_scalar(out=rs[:], in0=ss[:], scalar1=1.0 / D, scalar2=EPS, op0=ALU.mult, op1=ALU.add), r=[ssb], w=[rsb])
                S.op("act", lambda e: e.activation(out=rs[:], in_=rs[:], func=AF.Ln), r=[rsb], w=[rsb])
                S.op("act", lambda e: e.activation(out=rs[:], in_=rs[:], func=AF.Exp, scale=-0.5), r=[rsb], w=[rsb])
                for g in range(NG):
                    S.op("dve", (lambda e, g=g: e.scalar_tensor_tensor(out=xt[:, g, :], in0=xt[:, g, :], scalar=rs[:, g:g + 1], in1=self.gfin[:],
                                                                       op0=ALU.mult, op1=ALU.mult)), r=[xtb[g], rsb, self.gfinb], w=[xtb[g]])
                S.op("sp", (lambda e, rows=rows: e.dma_start(out=self.out[rows, :].rearrange("(g p) d -> p g d", p=128), in_=xt[:])),
                     r=xtb, w=[self.outb[t]], dma="xst")
            else:
                S.op("sp", (lambda e, rows=rows: e.dma_start(out=self.xs_w[rows, :].rearrange("(g p) d -> p g d", p=128), in_=xt[:])),
                     r=xtb, w=[self.xs_wb[t]], dma="xst")
        if r1 is not None:
            so = self.sx_out[si]
            S.op("sp", lambda e: e.dma_start(out=so.rearrange("p (h v) -> p h v", v=128), in_=self.S1[:]), r=self.S1b, w=[self.sxob[si]], dma="sout")
            uo = self.ut_out[si]
            S.op("sp", lambda e: e.dma_start(out=uo.rearrange("p (g t) -> p g t", t=16), in_=self.utail1[:]), r=[self.utail1b], w=[self.utob[si]], dma="uout")
            if self.fused:
                self.exchange(si)

    def exchange(self, si):
        raise NotImplementedError

    def build(self):
        nc = bass.Bass("TRN2", target_bir_lowering=False)
        self.nc = nc
        self.in_names = []
        self.stack = contextlib.ExitStack()
        NT, TOK = self.NT, self.TOK
        T = {}
        self.T = T
        T["x"] = self.dram_in("x", [TOK, D])
        T["mem"] = self.dram_in("mem", [NMEM, D])
        T["lb_logits"] = self.dram_in("lb_logits", [4, 512])
        T["pinv0"] = self.dram_in("pinv0", [4, 16])
        T["lsel"] = self.dram_in("lsel", [10, 4])
        T["isb"] = self.dram_in("isb", [1])
        for nm, shp in self.names2d:
            T[nm] = self.dram_in(nm, shp)
        nst = len(self.stages)
        self.outbufs = []
        self.xinb = [Buf("xin%d" % t) for t in range(NT)]
        if self.fused:
            raise NotImplementedError
        else:
            (r2, r1, fin) = self.stages[0]
            self.sx_in = [self.dram_in("s_in", [128, 512])]
            self.ut_in = [self.dram_in("ut_in", [128, 64])]
            self.sxb = [Buf("sxin")]
            self.utb = [Buf("utin")]
            if fin:
                self.out = nc.dram_tensor("out", [TOK, D], F32, kind="ExternalOutput").ap()
                self.outb = [Buf("out%d" % t) for t in range(NT)]
                self.outbufs += self.outb
            else:
                self.xs_w = nc.dram_tensor("x_out", [TOK, D], F32, kind="ExternalOutput").ap()
                self.xs_wb = [Buf("xo%d" % t) for t in range(NT)]
                self.outbufs += self.xs_wb
            self.xs = None
            self.xsb = None
            if r1 is not None:
                self.sx_out = [nc.dram_tensor("s_out", [128, 512], F32, kind="ExternalOutput").ap()]
                self.ut_out = [nc.dram_tensor("ut_out", [128, 64], F32, kind="ExternalOutput").ap()]
                self.sxob = [Buf("sxo")]
                self.utob = [Buf("uto")]
                self.outbufs += self.sxob + self.utob
        sb = self.sb
        self.xt = sb("xt", [128, NG, D], F32)
        self.xtb = [Buf("xt%d" % g) for g in range(NG)]
        self.junk = sb("junk", [128, D], BF16)
        self.ss = [sb("ss%d" % i, [128, NG], F32) for i in range(2)]
        self.ssb = [Buf("ss") for i in range(2)]
        self.rs = [sb("rs%d" % i, [128, NG], F32) for i in range(2)]
        self.rsb = [Buf("rs") for i in range(2)]
        self.xn = [sb("xn%d" % i, [128, D], BF16) for i in range(2)]
        self.xnb = [Buf("xn") for i in range(2)]
        self.hT = sb("hT", [128, 8, TT], BF16)
        self.hTb = Buf("hT")
        self.aT = sb("aT", [128, NFC, TT], BF16)
        self.aTflat = self.aT[:].rearrange("p a b -> p (a b)")
        self.aTb = [Buf("aT%d" % i) for i in range(NFC)]
        self.tmps = [sb("tmps%d" % i, [128, TT], F32) for i in range(2)]
        self.tmpsb = [Buf("tmps") for i in range(2)]
        self.NS = 5
        self.wslot = [sb("ws%d" % i, [128, 8, 512], BF16) for i in range(self.NS)]
        self.wslotb = [Buf("ws%d" % i) for i in range(self.NS)]
        self.identb = sb("identb", [128, 128], BF16)
        self.identf = sb("identf", [128, 128], F32)
        self.onesf = sb("onesf", [128, 128], F32)
        self.onesb = sb("onesb", [128, 128], BF16)
        self.onesrow = sb("onesrow", [128, TT], F32)
        self.mask = sb("mask", [128, 256], F32)
        self.cb = Buf("const")
        self.cb2 = Buf("const2")
        self.pinv0 = sb("pinv0", [128, 4, 16], F32)
        self.lsel = sb("lsel", [128, 10, 4], F32)
        self.isb = sb("isb", [128, 1], F32)
        self.prmS = sb("prmS", [128, 128], F32)
        self.prmSb = Buf("prmS")
        self.prmT = sb("prmT", [128, 128], F32)
        self.prmb = Buf("prmT")
        self.lbx = sb("lbx", [128, 16], F32)
        self.lbxb = Buf("lbx")
        self.lbxb2 = Buf("lbx2")
        self.lbd = sb("lbd", [128, 12], F32)
        self.lbc = sb("lbc", [128, 16], F32)
        self.lbcb = Buf("lbc")
        self.lbs = sb("lbs", [128, 24], F32)
        self.lbsb = Buf("lbs")
        self.abg = sb("abg", [128, NH, 3, NCH], F32)
        self.abgb = [Buf("abg%d" % h) for h in range(NH)]
        self.khatT = sb("khatT", [128, NH, TT], BF16)
        self.khTb = [Buf("khT%d" % h) for h in range(NH)]
        self.qhatT = sb("qhatT", [128, NH, TT], BF16)
        self.qhTb = [Buf("qhT%d" % h) for h in range(NH)]
        self.khtok = sb("khtok", [128, 2, NG, 512], BF16)
        self.khtokb = [Buf("khtok%d" % g) for g in range(NG)]
        self.vtok = sb("vtok", [128, NG, 512], BF16)
        self.vtokb = [Buf("vtok%d" % g) for g in range(NG)]
        self.S2 = sb("S2", [128, NH, 128], F32)
        self.S2b = [Buf("S2_%d" % h) for h in range(NH)]
        self.S1 = sb("S1", [128, NH, 128], F32)
        self.S1b = [Buf("S1_%d" % h) for h in range(NH)]
        self.Ssc = sb("Ssc", [128, NH, NCH, 128], BF16)
        self.Sscb = [[Buf("Ssc") for c in range(NCH)] for h in range(NH)]
        self.tmpd = [sb("tmpd%d" % i, [128, 128], F32) for i in range(2)]
        self.tmpdb = [Buf("tmpd") for i in range(2)]
        self.ATm = sb("ATm", [128, NH, 512], BF16)
        self.ATmb = [Buf("ATm%d" % h) for h in range(NH)]
        self.catT = sb("catT", [128, 8, TT], BF16)
        self.catb = [Buf("cat%d" % i) for i in range(8)]
        self.ubuf = sb("ubuf", [128, TT + 16], F32)
        self.ubufb = Buf("ubuf")
        self.halo = sb("halo", [128, 4, 16], F32)
        self.halob = [Buf("halo%d" % g) for g in range(4)]
        self.utail1 = sb("utail1", [128, 4, 16], F32)
        self.utail1b = Buf("utail1")
        self.pp = [sb("pp%d" % i, [128, TT + 16], F32) for i in range(2)]
        self.ppb = [Buf("pp") for i in range(2)]
        self.t16 = sb("t16", [128, 16], F32)
        self.t16b = Buf("t16")
        self.pT = sb("pT", [128, 4, TT], BF16)
        self.pTb = [Buf("pT%d" % g) for g in range(4)]
        self.poolw = sb("poolw", [128, 4, 128], BF16)
        self.poolwb = Buf("poolw")
        self.KT = sb("KT", [128, 8, NMEM], BF16)
        self.KTb = Buf("KT")
        self.Vm = sb("Vm", [128, 2, D], BF16)
        self.Vmb = Buf("Vm")
        self.PT = [sb("PT%d" % i, [128, 2, TT], BF16) for i in range(2)]
        self.PTb = [Buf("PT") for i in range(2)]
        self.gfin = sb("gfin", [128, D], F32)
        self.gfinb = Buf("gfin")
        self.ps = [self.stack.enter_context(nc.psum_tensor("ps%d" % i, [128, 512], F32)) for i in range(8)]
        self.psb = [Buf("ps%d" % i) for i in range(8)]

        self.wreqs = []
        dry = Sched(dry=True)
        self.program(dry)
        self.wplan = list(self.wreqs)
        S = Sched()
        self.program(S)
        S.finalize(nc, self.stack)
        with nc.Block() as block:
            @block.tensor
            def _(e):
                S.emit("pe", e)

            @block.scalar
            def _(e):
                S.emit("act", e)

            @block.vector
            def _(e):
                S.emit("dve", e)

            @block.gpsimd
            def _(e):
                S.emit("pool", e)

            @block.sync
            def _(e):
                S.emit("sp", e)
        self.stack.close()
        self.nops = {k: len(v) for k, v in S.ops.items()}
        return nc


P2_NAMES = [("mix_norm", [D]), ("w_in", [D, 2560]), ("hgrn_norm", [512]), ("pool_w", [4, 128, 128]), ("pool_scale", [512]),
            ("w_out", [D, D]), ("xa_norm", [D]), ("mem_norm", [D]), ("xa_wq", [D, D]), ("xa_wkv", [D, 2 * D]),
            ("xa_wo", [D, D]), ("ffn2_norm", [D]), ("ffn2_w1", [D, DFF]), ("ffn2_w3", [D, DFF]), ("ffn2_w2", [DFF, D])]
P1_NAMES = [("ffn1_norm", [D]), ("ffn1_w1", [D, DFF]), ("ffn1_w3", [D, DFF]), ("ffn1_w2", [DFF, D]), ("mix_norm", [D]),
            ("w_in", [D, 2560])]

_PROGS = {}


def get_prog(NT, kind):
    key = (NT, kind)
    if key not in _PROGS:
        names = []
        if kind == "first":
            st = (None, "b", False)
        elif kind == "mid":
            st = ("a", "b", False)
        else:
            st = ("a", None, True)
        if st[0]:
            names += [("%s_a" % n, s) for n, s in P2_NAMES]
        if st[1]:
            names += [("%s_b" % n, s) for n, s in P1_NAMES]
        if st[2]:
            names += [("final_norm", [D])]
        g = Gen(NT, [st], False, names, st[2])
        nc = g.build()
        _PROGS[key] = (g, nc)
    return _PROGS[key]


def run_unfused(inputs, NT, ncores):
    x = np.ascontiguousarray(inputs["x"], dtype=np.float32)
    B = x.shape[0]
    TOK = NT * TT
    L = inputs["w_in"].shape[0]
    xs = [np.ascontiguousarray(x[c // 2, (c % 2) * TOK:(c % 2 + 1) * TOK, :]) for c in range(ncores)]
    mems = [np.ascontiguousarray(inputs["mem"][c // 2], dtype=np.float32) for c in range(ncores)]
    wdw = np.array([2.0, 4.0, 8.0, 16.0], np.float32)
    pinv_a = np.stack([1.0 / np.minimum(np.arange(1, 17, dtype=np.float32), w) for w in wdw]).astype(np.float32)
    pinv_b = np.stack([np.full(16, 1.0 / w, np.float32) for w in wdw]).astype(np.float32)
    s_in = [np.zeros((128, 512), np.float32) for c in range(ncores)]
    ut_in = [np.zeros((128, 64), np.float32) for c in range(ncores)]
    out = None
    for si in range(L + 1):
        r2 = si - 1 if si >= 1 else None
        r1 = si if si < L else None
        kind = "first" if r2 is None else ("last" if r1 is None else "mid")
        g, nc = get_prog(NT, kind)
        lsel = np.zeros((10, 4), np.float32)
        if r2 is not None:
            lsel[0, r2] = 1.0
        if r1 is not None:
            lsel[1, r1] = 1.0
        in_maps = []
        for c in range(ncores):
            m = {"x": xs[c], "mem": mems[c], "lb_logits": np.ascontiguousarray(inputs["lb_logits"], dtype=np.float32),
                 "pinv0": pinv_a if c % 2 == 0 else pinv_b, "lsel": lsel,
                 "isb": np.array([float(c % 2)], np.float32), "s_in": s_in[c], "ut_in": ut_in[c]}
            if r2 is not None:
                for n, _ in P2_NAMES:
                    m["%s_a" % n] = np.ascontiguousarray(inputs[n][r2], dtype=np.float32)
            if r1 is not None:
                for n, _ in P1_NAMES:
                    m["%s_b" % n] = np.ascontiguousarray(inputs[n][r1], dtype=np.float32)
            if kind == "last":
                m["final_norm"] = np.ascontiguousarray(inputs["final_norm"], dtype=np.float32)
            in_maps.append(m)
        res = run_bass_kernel_spmd(nc, in_maps, core_ids=list(range(ncores)))
        rr = res.results
        if kind == "last":
            out = [np.asarray(rr[c]["out"]) for c in range(ncores)]
        else:
            xs = [np.asarray(rr[c]["x_out"]) for c in range(ncores)]
            s_in = [np.asarray(rr[c - 1]["s_out"]) if c % 2 == 1 else np.zeros((128, 512), np.float32) for c in range(ncores)]
            ut_in = [np.asarray(rr[c - 1]["ut_out"]) if c % 2 == 1 else np.zeros((128, 64), np.float32) for c in range(ncores)]
    full = np.empty((B, 2 * TOK, D), np.float32)
    for c in range(ncores):
        full[c // 2, (c % 2) * TOK:(c % 2 + 1) * TOK, :] = out[c]
    return full


def kernel(**inputs):
    return run_unfused(inputs, 8, 8)
```
